# Optimizing a Trainium2 kernel written in Bass

```python
import math
import jax
import jax.numpy as jnp
from jax import lax
import numpy as np

D_MODEL = 2048
BATCH = 2
SEQ = 4096
DEPTH = 1

CTX_LEN = 256
GRID_W = 64
S5_WIDTH = D_MODEL // 2
S5_GROUP = 16
S5_GROUPS = S5_WIDTH // S5_GROUP
S5_STATE = 64
RWKV_WIDTH = D_MODEL // 2
RWKV_HEAD = 64
RWKV_HEADS = RWKV_WIDTH // RWKV_HEAD
DECAY_LORA = 64
ICLR_LORA = 64
GATE_LORA = 128
RWKV_SHIFT_COLS = 3 * RWKV_WIDTH + 2 * DECAY_LORA + 2 * ICLR_LORA + GATE_LORA
IN_COLS = S5_WIDTH + RWKV_SHIFT_COLS + 2 * D_MODEL
D_FF = -(-(8 * D_MODEL) // (3 * 256)) * 256
N_MOD = 6
NORM_EPS = 1e-6
GN_EPS = 64e-5
DT_MIN = 1e-3
DT_MAX = 1e-1

kernel_name = 'hybrid_s5_rwkv7_flow_block'


def rms_norm(x, w):
    xf = x.astype(jnp.float32)
    y = xf * lax.rsqrt(jnp.mean(xf * xf, axis=-1, keepdims=True) + NORM_EPS)
    return (y * w.astype(jnp.float32)).astype(x.dtype)


def modulate(h, shift, scale):
    return h * (1 + scale) + shift


def grid_neighbour_mean(z):
    bsz, length, ch = z.shape
    rows = length // GRID_W
    g = z.reshape(bsz, rows, GRID_W, ch)
    p = jnp.pad(g, ((0, 0), (1, 1), (1, 1), (0, 0)))
    s = p[:, :-2, 1:-1] + p[:, 2:, 1:-1] + p[:, 1:-1, :-2] + p[:, 1:-1, 2:]
    ri = jnp.arange(rows)
    ci = jnp.arange(GRID_W)
    cnt = ((ri > 0).astype(z.dtype) + (ri < rows - 1).astype(z.dtype))[:, None] + ((ci > 0).astype(z.dtype) + (ci < GRID_W - 1).astype(z.dtype))[None, :]
    return (s / cnt[None, :, :, None]).reshape(bsz, length, ch)


def seq_neighbour_mean(z):
    length = z.shape[1]
    p = jnp.pad(z, ((0, 0), (1, 1), (0, 0)))
    i = jnp.arange(length)
    cnt = (i > 0).astype(z.dtype) + (i < length - 1).astype(z.dtype)
    return (p[:, :-2] + p[:, 2:]) / cnt[None, :, None]


def _complex(re, im):
    return lax.complex(re.astype(jnp.float32), im.astype(jnp.float32))


def s5_scan(bu, abar, s0):
    a = jnp.broadcast_to(abar, (bu.shape[0], 1) + abar.shape)

    def combine(e1, e2):
        a1, b1 = e1
        a2, b2 = e2
        return a1 * a2, a2 * b1 + b2

    a_cum, b_cum = lax.associative_scan(combine, (a, bu), axis=0)
    return b_cum + a_cum * s0[None]


def s5_mixer(u, lp, s0):
    bsz, length, _ = u.shape
    ug = jnp.swapaxes(u.astype(jnp.float32).reshape(bsz, length, S5_GROUPS, S5_GROUP), 0, 1)
    ugc = ug.astype(jnp.complex64)
    y = lp['s5_d'].astype(jnp.float32) * ug
    finals = []
    for di in range(2):
        a = _complex(lp['s5_a_re'][di], lp['s5_a_im'][di])
        dt = jnp.exp(lp['s5_log_dt'][di].astype(jnp.float32))[:, None]
        abar = jnp.exp(a * dt)
        bbar = ((abar - 1) / a)[..., None] * _complex(lp['s5_b_re'][di], lp['s5_b_im'][di])
        bu = jnp.einsum('lbgh,gph->lbgp', ugc, bbar)
        if di == 1:
            bu = bu[::-1]
        states = s5_scan(bu, abar, s0[di])
        finals.append(states[-1])
        if di == 1:
            states = states[::-1]
        cm = _complex(lp['s5_c_re'][di], lp['s5_c_im'][di])
        y = y + jnp.real(jnp.einsum('lbgp,ghp->lbgh', states, cm))
    y = jnp.swapaxes(y, 0, 1).reshape(bsz, length, S5_WIDTH)
    return y.astype(u.dtype), (finals[0], finals[1])


def rwkv_scan(r, w, k, v, kk, a, s0):
    def step(s, inp):
        r_t, w_t, k_t, v_t, kk_t, a_t = inp
        s_kk = jnp.einsum('bhvk,bhk->bhv', s, kk_t)
        s = s * w_t[:, :, None, :] - s_kk[..., None] * (kk_t * a_t)[:, :, None, :] + v_t[..., None] * k_t[:, :, None, :]
        return s, jnp.einsum('bhvk,bhk->bhv', s, r_t)

    s_fin, y = lax.scan(step, s0, (r, w, k, v, kk, a))
    return y, s_fin


def rwkv_mixer(z, lp, s0):
    bsz, length, _ = z.shape
    zf = z.astype(jnp.float32)
    rw = RWKV_WIDTH
    r = zf[..., :rw]
    k = zf[..., rw:2 * rw]
    v = zf[..., 2 * rw:3 * rw]
    o = 3 * rw
    wd = zf[..., o:o + 2 * DECAY_LORA].reshape(bsz, length, 2, DECAY_LORA)
    o += 2 * DECAY_LORA
    ad = zf[..., o:o + 2 * ICLR_LORA].reshape(bsz, length, 2, ICLR_LORA)
    o += 2 * ICLR_LORA
    gd = zf[..., o:o + GATE_LORA]
    g = jax.nn.sigmoid(gd) @ lp['rw_g2'].astype(jnp.float32)

    def heads(t):
        return jnp.swapaxes(t.reshape(bsz, length, RWKV_HEADS, RWKV_HEAD), 0, 1)

    kk = heads(k * lp['rw_k_k'].astype(jnp.float32))
    kk = kk * lax.rsqrt(jnp.sum(kk * kk, axis=-1, keepdims=True) + 1e-12)
    r_h = heads(r)
    v_h = heads(v)
    r_k = lp['rw_r_k'].astype(jnp.float32)
    ys = []
    bonuses = []
    finals = []
    for di in range(2):
        w = -jax.nn.softplus(-(lp['rw_w0'][di].astype(jnp.float32) + jnp.tanh(wd[:, :, di]) @ lp['rw_w2'][di].astype(jnp.float32))) - 0.5
        decay = jnp.exp(-jnp.exp(w))
        a = jax.nn.sigmoid(lp['rw_a0'][di].astype(jnp.float32) + ad[:, :, di] @ lp['rw_a2'][di].astype(jnp.float32))
        k_d = heads(k * (1 + (a - 1) * lp['rw_k_a'].astype(jnp.float32)))
        seqs = (r_h, heads(decay), k_d, v_h, kk, heads(a))
        if di == 1:
            seqs = tuple(t[::-1] for t in seqs)
        y_d, s_fin = rwkv_scan(*seqs, s0[di])
        if di == 1:
            y_d = y_d[::-1]
        ys.append(y_d)
        bonuses.append(jnp.sum(r_h * k_d * r_k, axis=-1, keepdims=True) * v_h)
        finals.append(s_fin)
    y = ys[0] + ys[1]
    mu = jnp.mean(y, axis=-1, keepdims=True)
    var = jnp.mean(jnp.square(y - mu), axis=-1, keepdims=True)
    y = (y - mu) * lax.rsqrt(var + GN_EPS)
    ln_w = lp['rw_ln_w'].astype(jnp.float32).reshape(RWKV_HEADS, RWKV_HEAD)
    ln_b = lp['rw_ln_b'].astype(jnp.float32).reshape(RWKV_HEADS, RWKV_HEAD)
    y = y * ln_w + ln_b + bonuses[0] + bonuses[1]
    y = jnp.swapaxes(y, 0, 1).reshape(bsz, length, rw) * g
    return y.astype(z.dtype), (finals[0], finals[1])


def token_mix(h, neighbour_mean, s5_init, rw_init, lp, with_output):
    z = jnp.einsum('bld,dc->blc', h, lp['w_in'])
    u_s5 = z[..., :S5_WIDTH]
    z_rw = z[..., S5_WIDTH:S5_WIDTH + RWKV_SHIFT_COLS]
    z_rw = z_rw + (neighbour_mean(z_rw) - z_rw) * lp['rw_mu']
    y_s5, s5_fin = s5_mixer(u_s5, lp, s5_init)
    y_rw, rw_fin = rwkv_mixer(z_rw, lp, rw_init)
    if not with_output:
        return None, s5_fin, rw_fin
    gates = jax.nn.sigmoid(z[..., S5_WIDTH + RWKV_SHIFT_COLS:])
    glu = jax.nn.gelu(y_s5) @ lp['s5_glu_w']
    s5_out = glu[..., :D_MODEL] * jax.nn.sigmoid(glu[..., D_MODEL:])
    rw_out = y_rw @ lp['rw_proj']
    merged = gates[..., :D_MODEL] * s5_out + gates[..., D_MODEL:] * rw_out
    return merged @ lp['w_o'], s5_fin, rw_fin


def swiglu(h, w13, w2):
    z = h @ w13
    return (jax.nn.silu(z[..., :D_FF]) * z[..., D_FF:]) @ w2


def setup_inputs(seed: int = 0) -> dict:
    key = jax.random.key(seed)
    ks = jax.random.split(key, 40)

    def nrm(k, shape, s):
        return jax.random.normal(k, shape, jnp.float32) * s

    dp = DEPTH
    g_, p_, hg = S5_GROUPS, S5_STATE, S5_GROUP
    a_im_base = math.pi * jnp.arange(p_, dtype=jnp.float32)
    return {
        'x': nrm(ks[0], (BATCH, SEQ, D_MODEL), 1.0),
        'c': nrm(ks[1], (BATCH, D_MODEL), 1.0),
        'ctx': nrm(ks[2], (BATCH, CTX_LEN, D_MODEL), 1.0),
        'c_ctx': nrm(ks[3], (D_MODEL,), 1.0),
        'ada_w': nrm(ks[4], (dp, D_MODEL, N_MOD * D_MODEL), 0.5 * D_MODEL ** -0.5),
        'ada_b': nrm(ks[5], (dp, N_MOD * D_MODEL), 0.02),
        'norm1_w': 1.0 + nrm(ks[6], (dp, D_MODEL), 0.02),
        'w_in': nrm(ks[7], (dp, D_MODEL, IN_COLS), D_MODEL ** -0.5),
        'rw_mu': jax.random.uniform(ks[8], (dp, RWKV_SHIFT_COLS), jnp.float32),
        's5_a_re': -0.5 + nrm(ks[9], (dp, 2, g_, p_), 0.01),
        's5_a_im': a_im_base + nrm(ks[10], (dp, 2, g_, p_), 0.01),
        's5_log_dt': jax.random.uniform(ks[11], (dp, 2, g_), jnp.float32, math.log(DT_MIN), math.log(DT_MAX)),
        's5_b_re': nrm(ks[12], (dp, 2, g_, p_, hg), (2 * hg) ** -0.5),
        's5_b_im': nrm(ks[13], (dp, 2, g_, p_, hg), (2 * hg) ** -0.5),
        's5_c_re': nrm(ks[14], (dp, 2, g_, hg, p_), (2 * p_) ** -0.5),
        's5_c_im': nrm(ks[15], (dp, 2, g_, hg, p_), (2 * p_) ** -0.5),
        's5_d': nrm(ks[16], (dp, g_, hg), 1.0),
        's5_glu_w': nrm(ks[17], (dp, S5_WIDTH, 2 * D_MODEL), S5_WIDTH ** -0.5),
        'rw_w0': jax.random.uniform(ks[18], (dp, 2, RWKV_WIDTH), jnp.float32, -6.0, -1.0),
        'rw_w2': nrm(ks[19], (dp, 2, DECAY_LORA, RWKV_WIDTH), 0.1 * DECAY_LORA ** -0.5),
        'rw_a0': nrm(ks[20], (dp, 2, RWKV_WIDTH), 0.1),
        'rw_a2': nrm(ks[21], (dp, 2, ICLR_LORA, RWKV_WIDTH), 0.1 * ICLR_LORA ** -0.5),
        'rw_g2': nrm(ks[22], (dp, GATE_LORA, RWKV_WIDTH), GATE_LORA ** -0.5),
        'rw_k_k': 0.85 + nrm(ks[23], (dp, RWKV_WIDTH), 0.02),
        'rw_k_a': 1.0 + nrm(ks[24], (dp, RWKV_WIDTH), 0.02),
        'rw_r_k': nrm(ks[25], (dp, RWKV_HEADS, RWKV_HEAD), 0.1),
        'rw_ln_w': 1.0 + nrm(ks[26], (dp, RWKV_WIDTH), 0.02),
        'rw_ln_b': nrm(ks[27], (dp, RWKV_WIDTH), 0.02),
        'rw_proj': nrm(ks[28], (dp, RWKV_WIDTH, D_MODEL), RWKV_WIDTH ** -0.5),
        'w_o': nrm(ks[29], (dp, D_MODEL, D_MODEL), D_MODEL ** -0.5),
        'norm2_w': 1.0 + nrm(ks[30], (dp, D_MODEL), 0.02),
        'ffn_w13': nrm(ks[31], (dp, D_MODEL, 2 * D_FF), D_MODEL ** -0.5),
        'ffn_w2': nrm(ks[32], (dp, D_FF, D_MODEL), D_FF ** -0.5),
        'norm_f': 1.0 + nrm(ks[33], (D_MODEL,), 0.02),
    }


def reference(x, c, ctx, c_ctx, ada_w, ada_b, norm1_w, w_in, rw_mu, s5_a_re, s5_a_im, s5_log_dt,
              s5_b_re, s5_b_im, s5_c_re, s5_c_im, s5_d, s5_glu_w, rw_w0, rw_w2, rw_a0, rw_a2, rw_g2,
              rw_k_k, rw_k_a, rw_r_k, rw_ln_w, rw_ln_b, rw_proj, w_o, norm2_w, ffn_w13, ffn_w2, norm_f):
    bsz = x.shape[0]
    s5_zero = jnp.zeros((bsz, S5_GROUPS, S5_STATE), jnp.complex64)
    rw_zero = jnp.zeros((bsz, RWKV_HEADS, RWKV_HEAD, RWKV_HEAD), jnp.float32)
    h_lat = x
    h_ctx = ctx
    for i in range(DEPTH):
        lp = {
            'w_in': w_in[i], 'rw_mu': rw_mu[i],
            's5_a_re': s5_a_re[i], 's5_a_im': s5_a_im[i], 's5_log_dt': s5_log_dt[i],
            's5_b_re': s5_b_re[i], 's5_b_im': s5_b_im[i], 's5_c_re': s5_c_re[i], 's5_c_im': s5_c_im[i],
            's5_d': s5_d[i], 's5_glu_w': s5_glu_w[i],
            'rw_w0': rw_w0[i], 'rw_w2': rw_w2[i], 'rw_a0': rw_a0[i], 'rw_a2': rw_a2[i], 'rw_g2': rw_g2[i],
            'rw_k_k': rw_k_k[i], 'rw_k_a': rw_k_a[i], 'rw_r_k': rw_r_k[i],
            'rw_ln_w': rw_ln_w[i], 'rw_ln_b': rw_ln_b[i], 'rw_proj': rw_proj[i], 'w_o': w_o[i],
        }
        last = i == DEPTH - 1
        mod = jax.nn.silu(c) @ ada_w[i] + ada_b[i]
        mod_c = jax.nn.silu(c_ctx) @ ada_w[i] + ada_b[i]
        sh1, sc1, g1, sh2, sc2, g2 = jnp.split(mod[:, None, :], N_MOD, axis=-1)
        sh1c, sc1c, g1c, sh2c, sc2c, g2c = jnp.split(mod_c, N_MOD, axis=-1)
        ctx_out, s5_ctx, rw_ctx = token_mix(modulate(rms_norm(h_ctx, norm1_w[i]), sh1c, sc1c), seq_neighbour_mean,
                                            (s5_zero, s5_zero), (rw_zero, rw_zero), lp, not last)
        lat_out, _, _ = token_mix(modulate(rms_norm(h_lat, norm1_w[i]), sh1, sc1), grid_neighbour_mean,
                                  s5_ctx, rw_ctx, lp, True)
        h_lat = h_lat + g1 * lat_out
        h_lat = h_lat + g2 * swiglu(modulate(rms_norm(h_lat, norm2_w[i]), sh2, sc2), ffn_w13[i], ffn_w2[i])
        if not last:
            h_ctx = h_ctx + g1c * ctx_out
            h_ctx = h_ctx + g2c * swiglu(modulate(rms_norm(h_ctx, norm2_w[i]), sh2c, sc2c), ffn_w13[i], ffn_w2[i])
    return rms_norm(h_lat, norm_f)
```

```python
from contextlib import ExitStack
import math
import numpy as np
import concourse.bass as bass
import concourse.mybir as mybir
from concourse.bass_utils import run_bass_kernel_spmd

F32 = mybir.dt.float32
BF16 = mybir.dt.bfloat16
ALU = mybir.AluOpType
AF = mybir.ActivationFunctionType
AX = mybir.AxisListType
NCORES = 8


class Prog:
    def __init__(self):
        self.nc = bass.Bass("TRN2", target_bir_lowering=False)
        self.ops = []
        self.es = ExitStack()
        self.n = 0

    def sb(self, shape, dt=F32):
        self.n += 1
        return self.es.enter_context(self.nc.sbuf_tensor(f"sb{self.n}", list(shape), dt))

    def ps(self, shape, dt=F32):
        self.n += 1
        return self.es.enter_context(self.nc.psum_tensor(f"ps{self.n}", list(shape), dt))

    def dram(self, name, shape, kind, dt=F32):
        return self.nc.dram_tensor(name, list(shape), dt, kind=kind).ap()

    def op(self, eng, fn):
        self.ops.append((eng, fn, False))

    def dma(self, out, in_, eng="sp"):
        self.ops.append((eng, lambda e: e.dma_start(out=out, in_=in_), True))

    def finish(self):
        nc = self.nc
        engs = ["sp", "act", "dve", "pool", "pe"]
        sems = {e: self.es.enter_context(nc.semaphore("s_" + e)) for e in engs}
        dsem = self.es.enter_context(nc.semaphore("s_dma"))
        cnt = {e: 0 for e in engs}
        dcnt = 0
        streams = {e: [] for e in engs}
        prev = None
        for eng, fn, isdma in self.ops:
            wait = None
            if prev is not None:
                pk, pe_, pv = prev
                if pk == "d":
                    wait = (dsem, pv)
                elif pe_ == "pe" and eng == "pe" and not isdma:
                    wait = None
                else:
                    wait = (sems[pe_], pv)
            if isdma:
                dcnt += 16
                streams[eng].append((wait, fn, (dsem, 16)))
                prev = ("d", eng, dcnt)
            else:
                cnt[eng] += 1
                streams[eng].append((wait, fn, (sems[eng], 1)))
                prev = ("c", eng, cnt[eng])
        final = prev
        block = self.es.enter_context(nc.Block())

        def emit(e, name):
            for wait, fn, inc in streams[name]:
                if wait is not None:
                    e.wait_ge(wait[0], wait[1])
                fn(e).then_inc(inc[0], inc[1])
            if name == "sp" and final is not None:
                if final[0] == "d":
                    e.wait_ge(dsem, final[2])
                elif final[1] != "sp":
                    e.wait_ge(sems[final[1]], final[2])

        @block.sync
        def _(e):
            emit(e, "sp")

        @block.scalar
        def _(e):
            emit(e, "act")

        @block.vector
        def _(e):
            emit(e, "dve")

        @block.gpsimd
        def _(e):
            emit(e, "pool")

        @block.tensor
        def _(e):
            emit(e, "pe")

        self.es.close()
        return nc


def _run(nc, in_maps, outs):
    res = run_bass_kernel_spmd(nc, in_maps, core_ids=list(range(NCORES)))
    return [[np.asarray(r[o]) for o in outs] for r in res.results]


_ACTS = {None: None, "sigmoid": AF.Sigmoid, "silu": AF.Silu, "gelu": AF.Gelu, "tanh": AF.Tanh,
         "exp": AF.Exp, "ln": AF.Ln, "sin": AF.Sin, "square": AF.Square, "copy": AF.Copy,
         "identity": AF.Identity, "sqrt": AF.Sqrt}
_ALUS = {"add": ALU.add, "sub": ALU.subtract, "mul": ALU.mult, "max": ALU.max, "min": ALU.min,
         "pow": ALU.pow, "mod": ALU.mod, "div": ALU.divide, "is_gt": ALU.is_gt, "is_lt": ALU.is_lt}


def _build_bmm(bc, K, M, N, act, fp32, a_act=None, epi=(), mask=False):
    p = Prog()
    Kp = min(K, 128)
    KT = (K + 127) // 128
    assert K == Kp * KT
    at = p.dram("at", [bc, K, M], "ExternalInput")
    b = p.dram("b", [bc, K, N], "ExternalInput")
    c = p.dram("c", [bc, M, N], "ExternalOutput")
    mk = p.dram("mask", [M, N], "ExternalInput") if mask else None
    NB = 512
    cdt = F32 if fp32 else BF16
    a32 = p.sb([Kp, KT, 128])
    b32 = p.sb([Kp, KT, NB])
    a16 = p.sb([Kp, KT, 128], cdt) if not fp32 else a32
    b16 = p.sb([Kp, KT, NB], cdt) if not fp32 else b32
    ot = p.sb([128, NB])
    mt_ = p.sb([128, NB]) if mask else None
    pt = p.ps([128, NB])
    for bi in range(bc):
        for n0 in range(0, N, NB):
            nw = min(NB, N - n0)
            p.dma(b32[:, :, :nw], b[bi, :, n0:n0 + nw].rearrange("(kt p) n -> p kt n", p=Kp))
            if not fp32:
                p.op("pool", lambda e, nw=nw: e.tensor_copy(out=b16[:, :, :nw], in_=b32[:, :, :nw]))
            for m0 in range(0, M, 128):
                mw = min(128, M - m0)
                p.dma(a32[:, :, :mw], at[bi, :, m0:m0 + mw].rearrange("(kt p) m -> p kt m", p=Kp))
                if a_act is not None:
                    p.op("act", lambda e, mw=mw: e.activation(out=a16[:, :, :mw], in_=a32[:, :, :mw], func=_ACTS[a_act]))
                elif not fp32:
                    p.op("dve", lambda e, mw=mw: e.tensor_copy(out=a16[:, :, :mw], in_=a32[:, :, :mw]))
                for kt in range(KT):
                    p.op("pe", lambda e, kt=kt, mw=mw, nw=nw: e.matmul(
                        pt[:mw, :nw], a16[:, kt, :mw], b16[:, kt, :nw], start=(kt == 0), stop=(kt == KT - 1)))
                if mask:
                    p.dma(mt_[:mw, :nw], mk[m0:m0 + mw, n0:n0 + nw])
                    p.op("dve", lambda e, mw=mw, nw=nw: e.tensor_tensor(out=ot[:mw, :nw], in0=pt[:mw, :nw], in1=mt_[:mw, :nw], op=ALU.mult))
                elif epi:
                    for ei, (fn_, sc_, bi_) in enumerate(epi):
                        src = pt if ei == 0 else ot
                        p.op("act", lambda e, mw=mw, nw=nw, src=src, fn_=fn_, sc_=sc_, bi_=bi_: e.activation(
                            out=ot[:mw, :nw], in_=src[:mw, :nw], func=_ACTS[fn_], scale=float(sc_), bias=float(bi_)))
                elif act is None:
                    p.op("dve", lambda e, mw=mw, nw=nw: e.tensor_copy(out=ot[:mw, :nw], in_=pt[:mw, :nw]))
                else:
                    p.op("act", lambda e, mw=mw, nw=nw: e.activation(out=ot[:mw, :nw], in_=pt[:mw, :nw], func=_ACTS[act]))
                p.dma(c[bi, m0:m0 + mw, n0:n0 + nw], ot[:mw, :nw])
    return p.finish()


_cache = {}


def dev_bmm(AT, B, act=None, fp32=False, a_act=None, epi=(), mask=None):
    AT = np.ascontiguousarray(AT, np.float32)
    B = np.ascontiguousarray(B, np.float32)
    nb, K, M = AT.shape
    N = B.shape[2]
    if nb % NCORES == 0:
        bc = nb // NCORES
        key = ("bmm", bc, K, M, N, act, fp32, a_act, tuple(epi), mask is not None)
        if key not in _cache:
            _cache[key] = _build_bmm(bc, K, M, N, act, fp32, a_act, tuple(epi), mask is not None)
        maps = [{"at": AT[i * bc:(i + 1) * bc], "b": B[i * bc:(i + 1) * bc]} for i in range(NCORES)]
        if mask is not None:
            for m_ in maps:
                m_["mask"] = np.ascontiguousarray(mask, np.float32)
        r = _run(_cache[key], maps, ["c"])
        return np.concatenate([x[0] for x in r], 0)
    assert nb == 1
    Mp = -(-M // (128 * NCORES)) * 128 * NCORES
    Mc = Mp // NCORES
    ATp = np.zeros((K, Mp), np.float32)
    ATp[:, :M] = AT[0]
    assert mask is None
    key = ("bmm", 1, K, Mc, N, act, fp32, a_act, tuple(epi), False)
    if key not in _cache:
        _cache[key] = _build_bmm(1, K, Mc, N, act, fp32, a_act, tuple(epi))
    maps = [{"at": np.ascontiguousarray(ATp[None, :, i * Mc:(i + 1) * Mc]), "b": B} for i in range(NCORES)]
    r = _run(_cache[key], maps, ["c"])
    return np.concatenate([x[0][0] for x in r], 0)[None, :M]


def dev_mm(A, B, act=None, fp32=False, a_act=None, epi=()):
    return dev_bmm(np.ascontiguousarray(A.T)[None], B[None], act, fp32, a_act, epi)[0]


def _build_ew(Rc, F, prog, in_names, out_names, widths, rowvec=()):
    p = Prog()
    ins = {n: p.dram("i_" + n, [1 if n in rowvec else Rc, widths[n]], "ExternalInput") for n in in_names}
    outs = {n: p.dram("o_" + n, [Rc, widths[n]], "ExternalOutput") for n in out_names}
    regs = {}

    def reg(n, w=None):
        if n not in regs:
            regs[n] = p.sb([128, w if w is not None else widths.get(n, F)])
        return regs[n]

    for n in in_names:
        reg(n)
    for ins_ in prog:
        dst = ins_[1]
        if dst not in regs:
            w = 1 if ins_[0] in ("rsum", "rmax") else None
            if ins_[0] in ("tsp", "tt", "act", "ts", "recip", "rint") and dst not in widths:
                w = regs[ins_[2]].shape[1] if ins_[2] in regs else None
            reg(dst, w)
            widths.setdefault(dst, regs[dst].shape[1])
    for t in range(Rc // 128):
        rs = slice(t * 128, (t + 1) * 128)
        for n in in_names:
            if n in rowvec:
                if t == 0:
                    p.dma(regs[n][:, :], ins[n][0:1, :].partition_broadcast(128))
            else:
                p.dma(regs[n][:, :], ins[n][rs, :])
        for ins_ in prog:
            k = ins_[0]
            if k == "act":
                _, d, s, func, scale, bias = ins_
                p.op("act", lambda e, d=d, s=s, func=func, scale=scale, bias=bias: e.activation(
                    out=regs[d][:, :], in_=regs[s][:, :], func=_ACTS[func], scale=float(scale), bias=float(bias)))
            elif k == "tt":
                _, d, a, b_, o = ins_
                p.op("dve", lambda e, d=d, a=a, b_=b_, o=o: e.tensor_tensor(
                    out=regs[d][:, :], in0=regs[a][:, :], in1=regs[b_][:, :], op=_ALUS[o]))
            elif k == "ts":
                _, d, a, s1, o0, s2, o1 = ins_
                p.op("dve", lambda e, d=d, a=a, s1=s1, o0=o0, s2=s2, o1=o1: e.tensor_scalar(
                    out=regs[d][:, :], in0=regs[a][:, :], scalar1=float(s1), scalar2=float(s2),
                    op0=_ALUS[o0], op1=_ALUS[o1]))
            elif k == "tsp":
                _, d, a, s, o = ins_
                p.op("dve", lambda e, d=d, a=a, s=s, o=o: e.tensor_scalar(
                    out=regs[d][:, :], in0=regs[a][:, :], scalar1=regs[s][:, 0:1], scalar2=None, op0=_ALUS[o]))
            elif k == "rint":
                _, d, s = ins_
                if "__int" not in regs:
                    regs["__int"] = p.sb([128, F], mybir.dt.int32)
                wd_ = regs[s].shape[1]
                p.op("dve", lambda e, s=s, wd_=wd_: e.tensor_copy(out=regs["__int"][:, :wd_], in_=regs[s][:, :]))
                p.op("dve", lambda e, d=d, wd_=wd_: e.tensor_copy(out=regs[d][:, :], in_=regs["__int"][:, :wd_]))
            elif k == "recip":
                _, d, s = ins_
                p.op("dve", lambda e, d=d, s=s: e.reciprocal(out=regs[d][:, :], in_=regs[s][:, :]))
            elif k == "rsum":
                _, d, s = ins_
                p.op("dve", lambda e, d=d, s=s: e.tensor_reduce(
                    out=regs[d][:, :], in_=regs[s][:, :], axis=AX.X, op=ALU.add))
            elif k == "scan":
                _, d, d0, d1 = ins_
                p.op("dve", lambda e, d=d, d0=d0, d1=d1: e.tensor_tensor_scan(
                    out=regs[d][:, :], data0=regs[d0][:, :], data1=regs[d1][:, :], initial=0.0,
                    op0=ALU.mult, op1=ALU.add))
            else:
                raise ValueError(k)
        for n in out_names:
            p.dma(outs[n][rs, :], regs[n][:, :])
    return p.finish()


def dev_ew(prog, inputs, out_names, out_widths=None):
    prog = tuple(tuple(x) for x in prog)
    names = sorted(inputs)
    R = max(v.shape[0] for v in inputs.values())
    rowvec = tuple(n for n in names if inputs[n].shape[0] == 1 and R > 1)
    F = max(v.shape[1] for v in inputs.values())
    widths = {n: inputs[n].shape[1] for n in names}
    for n in out_names:
        widths[n] = (out_widths or {}).get(n, F)
    Rp = -(-R // (128 * NCORES)) * 128 * NCORES
    Rc = Rp // NCORES
    key = ("ew", Rc, F, prog, tuple(names), tuple(out_names), tuple(sorted(widths.items())), rowvec)
    if key not in _cache:
        _cache[key] = _build_ew(Rc, F, prog, names, list(out_names), dict(widths), rowvec)
    maps = []
    padded = {}
    for n in names:
        if n in rowvec:
            continue
        a = np.zeros((Rp, inputs[n].shape[1]), np.float32)
        a[:R] = inputs[n]
        padded[n] = a
    for i in range(NCORES):
        m = {"i_" + n: np.ascontiguousarray(padded[n][i * Rc:(i + 1) * Rc]) for n in padded}
        for n in rowvec:
            m["i_" + n] = np.ascontiguousarray(inputs[n], np.float32)
        maps.append(m)
    r = _run(_cache[key], maps, ["o_" + n for n in out_names])
    return [np.concatenate([r[i][j] for i in range(NCORES)], 0)[:R] for j in range(len(out_names))]


def _build_rwkv(nch, NCH):
    p = Prog()
    fm = p.dram("fm", [nch, NCH, 64, 4, 128], "ExternalInput")
    tm = p.dram("tm", [nch, NCH, 128, 3, 64], "ExternalInput")
    gt = p.dram("gt", [nch, NCH, 64, 1], "ExternalInput")
    cst = p.dram("cst", [128, 4, 128], "ExternalInput")
    yo = p.dram("y", [nch, NCH, 128, 64], "ExternalOutput")
    C = p.sb([128, 4, 128])
    fm32 = p.sb([64, 4, 128]); fm16 = p.sb([64, 4, 128], BF16)
    tm32 = p.sb([128, 3, 64]); tm16 = p.sb([128, 3, 64], BF16)
    g = p.sb([64, 1])
    P16 = p.sb([128, 128], BF16); PT16 = p.sb([128, 128], BF16)
    Pn16 = p.sb([128, 128], BF16); PTn16 = p.sb([128, 128], BF16)
    Y32 = p.sb([128, 128]); Y16 = p.sb([128, 128], BF16)
    MT16 = p.sb([128, 128], BF16); ArT16 = p.sb([128, 128], BF16); AaT16 = p.sb([128, 128], BF16)
    W16 = p.sb([128, 64], BF16); U16 = p.sb([128, 64], BF16); yt = p.sb([128, 64])
    H32 = p.sb([64, 64]); H16 = p.sb([64, 64], BF16)
    ps = p.ps([128, 128]); ps2 = p.ps([128, 64]); ps3 = p.ps([64, 64])
    p.dma(C[:, :, :], cst[:, :, :])
    RH, KH, AH, BH = 0, 1, 2, 3

    def mm(out, lhsT, rhs, start=True, stop=True):
        p.op("pe", lambda e: e.matmul(out, lhsT, rhs, start=start, stop=stop))

    def masked(dst, mi):
        p.op("dve", lambda e: e.tensor_tensor(out=dst[:, :], in0=ps[:, :], in1=C[:, mi, :], op=ALU.mult))

    for ci in range(nch):
        p.op("dve", lambda e: e.memset(H32[:, :], 0.0))
        p.op("dve", lambda e: e.memset(H16[:, :], 0.0))
        for c in range(NCH):
            p.dma(fm32[:, :, :], fm[ci, c])
            p.dma(tm32[:, :, :], tm[ci, c])
            p.dma(g[:, :], gt[ci, c])
            p.op("dve", lambda e: e.tensor_copy(out=fm16[:, :, :], in_=fm32[:, :, :]))
            p.op("pool", lambda e: e.tensor_copy(out=tm16[:, :, :], in_=tm32[:, :, :]))
            mm(ps[:, :], fm16[:, AH, :], fm16[:, BH, :]); masked(PT16, 0)
            p.op("dve", lambda e: e.tensor_tensor(out=Y32[:, :], in0=PT16[:, :], in1=C[:, 3, :], op=ALU.add))
            p.op("dve", lambda e: e.tensor_copy(out=Y16[:, :], in_=Y32[:, :]))
            mm(ps[:, :], fm16[:, BH, :], fm16[:, AH, :]); masked(P16, 1)
            mm(ps[:, :], fm16[:, KH, :], fm16[:, BH, :]); masked(MT16, 0)
            mm(ps[:, :], fm16[:, KH, :], fm16[:, RH, :]); masked(ArT16, 2)
            mm(ps[:, :], fm16[:, AH, :], fm16[:, RH, :]); masked(AaT16, 2)
            P, PT, Pn, PTn = P16, PT16, Pn16, PTn16
            for lvl in range(6):
                if lvl < 5:
                    mm(ps[:, :], P[:, :], PT[:, :])
                    p.op("act", lambda e, PTn=PTn: e.copy(out=PTn[:, :], in_=ps[:, :]))
                mm(ps[:, :], PT[:, :], P[:, :])
                p.op("act", lambda e, Pn=Pn: e.copy(out=Pn[:, :], in_=ps[:, :]))
                P, Pn = Pn, P
                PT, PTn = PTn, PT
                mm(ps[:, :], P[:, :], Y16[:, :])
                p.op("dve", lambda e: e.tensor_tensor(out=Y32[:, :], in0=ps[:, :], in1=Y32[:, :], op=ALU.add))
                p.op("dve", lambda e: e.tensor_copy(out=Y16[:, :], in_=Y32[:, :]))
            mm(ps2[:, :], fm16[:, BH, :], H16[:, :], True, False)
            mm(ps2[:, :], MT16[:, :], tm16[:, 0, :], False, True)
            p.op("act", lambda e: e.copy(out=W16[:, :], in_=ps2[:, :]))
            mm(ps2[:, :], Y16[:, :], W16[:, :])
            p.op("act", lambda e: e.copy(out=U16[:, :], in_=ps2[:, :]))
            mm(ps2[:, :], fm16[:, RH, :], H16[:, :], True, False)
            mm(ps2[:, :], ArT16[:, :], tm16[:, 0, :], False, False)
            mm(ps2[:, :], AaT16[:, :], U16[:, :], False, True)
            p.op("act", lambda e: e.copy(out=yt[:, :], in_=ps2[:, :]))
            p.dma(yo[ci, c], yt[:, :])
            mm(ps3[:, :], tm16[:, 1, :], tm16[:, 0, :], True, False)
            mm(ps3[:, :], tm16[:, 2, :], U16[:, :], False, True)
            p.op("dve", lambda e: e.scalar_tensor_tensor(out=H32[:, :], in0=H32[:, :], scalar=g[:, 0:1],
                                                          in1=ps3[:, :], op0=ALU.mult, op1=ALU.add))
            p.op("dve", lambda e: e.tensor_copy(out=H16[:, :], in_=H32[:, :]))
    return p.finish()


def _rwkv_consts():
    i = np.arange(128)
    su = (i[:, None] < i[None, :]).astype(np.float32)
    return np.ascontiguousarray(np.stack([su, su.T, (i[:, None] <= i[None, :]).astype(np.float32),
                                          np.eye(128, dtype=np.float32)], 1))


def dev_rwkv(FM, TM, GT):
    nchains, NCH = FM.shape[:2]
    assert nchains % NCORES == 0
    nch = nchains // NCORES
    key = ("rwkv", nch, NCH)
    if key not in _cache:
        _cache[key] = _build_rwkv(nch, NCH)
    cst = _rwkv_consts()
    maps = [{"fm": np.ascontiguousarray(FM[i * nch:(i + 1) * nch], np.float32),
             "tm": np.ascontiguousarray(TM[i * nch:(i + 1) * nch], np.float32),
             "gt": np.ascontiguousarray(GT[i * nch:(i + 1) * nch], np.float32), "cst": cst} for i in range(NCORES)]
    r = _run(_cache[key], maps, ["y"])
    return np.concatenate([x[0] for x in r], 0)


D = 2048
_NORM = (("tt", "q", "x", "x", "mul"), ("rsum", "ss", "q"), ("ts", "r", "ss", 1.0 / D, "mul", 1e-6, "add"),
         ("act", "r", "r", "sqrt", 1.0, 0.0), ("recip", "r", "r"), ("tsp", "q", "x", "r", "mul"),
         ("tt", "q", "q", "sc", "mul"), ("tt", "h", "q", "sh", "add"))


def _adaln(c, c_ctx, ada_w, ada_b, norm1_w, norm2_w):
    c3 = np.stack([c[0], c[1], c_ctx]).astype(np.float32)
    (c3s,) = dev_ew([("act", "o", "x", "silu", 1.0, 0.0)], {"x": c3}, ["o"])
    ATc = np.zeros((NCORES, D, 128), np.float32)
    ATc[:, :, :3] = c3s.T[None]
    Bc = np.ascontiguousarray(ada_w[0].reshape(D, NCORES, 6 * D // NCORES).transpose(1, 0, 2))
    mod = dev_bmm(ATc, Bc, fp32=True)[:, :3, :].transpose(1, 0, 2).reshape(18, D)
    b18 = np.tile(ada_b[0].reshape(6, D), (3, 1))
    w18 = np.ones((18, D), np.float32)
    w18[1::6] = norm1_w[0]
    w18[4::6] = norm2_w[0]
    f, s = dev_ew([("tt", "f", "m", "b", "add"), ("ts", "g", "f", 1.0, "add", 1.0, "mul"), ("tt", "s", "g", "w", "mul")],
                  {"m": mod, "b": b18, "w": w18}, ["f", "s"])
    return f.reshape(3, 6, D), s.reshape(3, 6, D)


PI = math.pi
_OFF = 2 * PI * 128


def _seq(lat_b, ctx_b, d):
    if d == 0:
        return np.concatenate([ctx_b, lat_b], 0)
    return np.concatenate([ctx_b[::-1], lat_b[::-1]], 0)


def _sincos(ang, sn, cs, y):
    def one(dst, off):
        return [("ts", y, ang, off, "add", 1.0 / (2 * PI), "mul"), ("rint", "kq", y), ("ts", "kq", "kq", -2 * PI, "mul", off, "add"),
                ("tt", y, ang, "kq", "add"), ("ts", "kq", y, PI, "is_gt", 2 * PI, "mul"), ("tt", y, y, "kq", "sub"),
                ("ts", "kq", y, -PI, "is_lt", 2 * PI, "mul"), ("tt", y, y, "kq", "add"), ("act", dst, y, "sin", 1.0, 0.0)]
    return one(sn, _OFF) + one(cs, _OFF + 0.5 * PI)


_BB = ([("act", "dt", "ldt", "exp", 1.0, 0.0), ("tt", "lr", "ar", "dt", "mul"), ("tt", "li", "ai", "dt", "mul"),
        ("act", "mag", "lr", "exp", 1.0, 0.0)] + _sincos("li", "sn", "cs", "y") +
       [("tt", "abr", "mag", "cs", "mul"), ("ts", "abr", "abr", -1.0, "add", 1.0, "mul"), ("tt", "abi", "mag", "sn", "mul"),
        ("tt", "d1", "ar", "ar", "mul"), ("tt", "d2", "ai", "ai", "mul"), ("tt", "den", "d1", "d2", "add"), ("recip", "den", "den"),
        ("tt", "ir", "ar", "den", "mul"), ("tt", "ii", "ai", "den", "mul"), ("ts", "ii", "ii", -1.0, "mul", 0.0, "add"),
        ("tt", "t1", "abr", "ir", "mul"), ("tt", "t2", "abi", "ii", "mul"), ("tt", "cr", "t1", "t2", "sub"),
        ("tt", "t1", "abr", "ii", "mul"), ("tt", "t2", "abi", "ir", "mul"), ("tt", "ci", "t1", "t2", "add"),
        ("tsp", "p1", "br", "cr", "mul"), ("tsp", "p2", "bi", "ci", "mul"), ("tt", "bbr", "p1", "p2", "sub"),
        ("tsp", "p1", "br", "ci", "mul"), ("tsp", "p2", "bi", "cr", "mul"), ("tt", "bbi", "p1", "p2", "add"),
        ("ts", "nbbi", "bbi", -1.0, "mul", 0.0, "add"), ("ts", "ncim", "cim", -1.0, "mul", 0.0, "add"),
        ("ts", "nli", "li", -1.0, "mul", 0.0, "add"),
        ("ts", "lr32", "lr", 32.0, "mul", 0.0, "add"), ("act", "rho", "lr32", "exp", 1.0, 0.0),
        ("ts", "phi", "li", 32.0, "mul", 0.0, "add"), ("ts", "kq", "phi", 1.0 / (2 * PI), "mul", 0.0, "add"), ("rint", "kq", "kq"),
        ("ts", "kq", "kq", -2 * PI, "mul", 0.0, "add"), ("tt", "phi", "phi", "kq", "add")])

_CPOW = ([("tsp", "t", "kk", "lr", "mul"), ("act", "mag", "t", "exp", 1.0, 0.0), ("tsp", "ang", "kk", "li", "mul")] +
         _sincos("ang", "sn", "cs", "y") +
         [("tt", "pr", "mag", "cs", "mul"), ("tt", "pi", "mag", "sn", "mul"),
          ("tt", "t1", "pr", "cr", "mul"), ("tt", "t2", "pi", "ci", "mul"), ("tt", "ore", "t1", "t2", "sub"),
          ("tt", "t1", "pr", "ci", "mul"), ("tt", "t2", "pi", "cr", "mul"), ("tt", "oim", "t1", "t2", "add")])

_CARRY = (("tt", "a", "er", "cs", "mul"), ("tt", "b", "ei", "sn", "mul"), ("tt", "xr", "a", "b", "add"),
          ("tt", "a", "ei", "cs", "mul"), ("tt", "b", "er", "sn", "mul"), ("tt", "xi", "a", "b", "sub"),
          ("scan", "tr", "rho", "xr"), ("scan", "ti", "rho", "xi"),
          ("tt", "a", "tr", "cs", "mul"), ("tt", "b", "ti", "sn", "mul"), ("tt", "sr", "a", "b", "sub"),
          ("tt", "a", "ti", "cs", "mul"), ("tt", "b", "tr", "sn", "mul"), ("tt", "si", "a", "b", "add"))

_S5FIN = (("tt", "ys", "u", "dv", "mul"), ("tt", "ys", "ys", "yf", "add"), ("tt", "ys", "ys", "yb", "add"),
          ("tt", "q", "ys", "ys", "mul"), ("tt", "q", "q", "ys", "mul"), ("ts", "q", "q", 0.044715, "mul", 0.0, "add"),
          ("tt", "q", "q", "ys", "add"), ("act", "q", "q", "tanh", 0.7978845608028654, 0.0),
          ("ts", "q", "q", 1.0, "add", 0.5, "mul"), ("tt", "gs", "q", "ys", "mul"))


def _cpow(lr, li, kk, cr, ci):
    return dev_ew(_CPOW, {"lr": lr, "li": li, "kk": kk[None].astype(np.float32), "cr": cr, "ci": ci}, ["ore", "oim"])


def s5_branch(u_lat, u_ctx, inp):
    G, PS, HG, T1 = 64, 64, 16, 32
    R = 2 * G * PS
    col = lambda a: np.ascontiguousarray(a, np.float32).reshape(R, -1)
    ar = col(inp["s5_a_re"][0]); ai = col(inp["s5_a_im"][0])
    ldt = col(np.repeat(inp["s5_log_dt"][0][:, :, None], PS, 2))
    br = col(inp["s5_b_re"][0]); bi = col(inp["s5_b_im"][0])
    cre = col(inp["s5_c_re"][0].transpose(0, 1, 3, 2)); cim = col(inp["s5_c_im"][0].transpose(0, 1, 3, 2))
    outs = ["lr", "li", "nli", "bbr", "bbi", "nbbi", "ncim", "lr32", "rho", "phi"]
    w = {n: 1 for n in ("lr", "li", "nli", "lr32", "rho", "phi")}
    w.update({n: 16 for n in ("bbr", "bbi", "nbbi", "ncim")})
    o = dict(zip(outs, dev_ew(_BB, {"ar": ar, "ai": ai, "ldt": ldt, "br": br, "bi": bi, "cim": cim}, outs, w)))
    jj = np.repeat(np.arange(T1), HG).astype(np.float32)
    tile = lambda a: np.tile(a, (1, T1))
    xr_, xi_ = _cpow(o["lr"], o["nli"], -jj, tile(o["bbr"]), tile(o["nbbi"]))
    zr_, zi_ = _cpow(o["lr"], o["li"], jj, tile(cre), tile(cim))
    er_, ei_ = _cpow(o["lr"], o["li"], (T1 - 1) - jj, tile(o["bbr"]), tile(o["bbi"]))
    or_, oi_ = _cpow(o["lr"], o["nli"], jj + 1, tile(cre), tile(o["ncim"]))
    kc = np.zeros(512, np.float32); kc[:136] = np.arange(136)
    cs_, sn_ = _cpow(np.zeros((R, 1), np.float32), o["phi"], kc, np.ones((R, 512), np.float32), np.zeros((R, 512), np.float32))
    cs_, sn_ = cs_[:, :136], sn_[:, :136]
    st = lambda a_, b_: np.concatenate([a_.reshape(2 * G, PS, -1), b_.reshape(2 * G, PS, -1)], 1)
    mask = (jj[None, :] >= jj[:, None]).astype(np.float32)
    toep = dev_bmm(st(xr_, xi_), st(zr_, zi_), mask=mask)
    U = np.zeros((2, G, T1 * HG, 2, 136), np.float32)
    for b in range(2):
        for d in range(2):
            sq = _seq(u_lat[b], u_ctx[b], d).reshape(136, T1, G, HG)
            U[d, :, :, b, :] = sq.transpose(2, 1, 3, 0).reshape(G, T1 * HG, 136)
    U = U.reshape(2 * G, T1 * HG, 272)
    ecoefT = np.ascontiguousarray(st(er_, ei_).transpose(0, 2, 1))
    E = dev_bmm(ecoefT, U)
    E = E.reshape(2 * G, 2, PS, 2, 136)
    rows = lambda a: np.ascontiguousarray(a.transpose(0, 1, 2, 3)).reshape(-1, 136)
    er = rows(E[:, 0]); ei = rows(E[:, 1])
    rep2 = lambda a: np.repeat(a, 2, 0)
    sr, si = dev_ew(_CARRY, {"er": er, "ei": ei, "cs": rep2(cs_), "sn": rep2(sn_),
                             "rho": np.repeat(rep2(o["rho"]), 136, 1)}, ["sr", "si"])
    S = np.stack([sr.reshape(2 * G, PS, 2, 136), si.reshape(2 * G, PS, 2, 136)], 1)
    Sp = np.zeros_like(S); Sp[..., 1:] = S[..., :-1]
    Sp = Sp.reshape(2 * G, 2 * PS, 272)
    AT = np.concatenate([toep, st(or_, oi_)], 1)
    Y = dev_bmm(AT, np.concatenate([U, Sp], 1))
    Y = Y.reshape(2, G, T1, HG, 2, 136)
    ydir = np.zeros((2, 2, 4096, 1024), np.float32)
    for b in range(2):
        for d in range(2):
            sq = Y[d, :, :, :, b, :].transpose(3, 1, 0, 2).reshape(4352, 1024)[256:]
            ydir[b, d] = sq if d == 0 else sq[::-1]
    gs, ys = dev_ew(_S5FIN, {"u": u_lat.reshape(8192, 1024), "yf": ydir[:, 0].reshape(8192, 1024),
                             "yb": ydir[:, 1].reshape(8192, 1024), "dv": inp["s5_d"][0].reshape(1, 1024)}, ["gs", "ys"])
    return gs.reshape(2, 4096, 1024), ys.reshape(2, 4096, 1024)


_RWP1 = (("tt", "y", "yf", "yb", "add"), ("rsum", "m", "y"), ("ts", "m", "m", -1.0 / 64, "mul", 0.0, "add"),
         ("tsp", "yc", "y", "m", "add"), ("tt", "q", "yc", "yc", "mul"), ("rsum", "v", "q"),
         ("ts", "v", "v", 1.0 / 64, "mul", 64e-5, "add"), ("act", "v", "v", "sqrt", 1.0, 0.0), ("recip", "v", "v"),
         ("tsp", "yn", "yc", "v", "mul"), ("tt", "ssum", "sf", "sb", "add"))
_RWP2 = (("tt", "a", "yn", "lw", "mul"), ("tt", "a", "a", "lb", "add"), ("tt", "b", "ss", "v", "mul"),
         ("tt", "a", "a", "b", "add"), ("tt", "o", "a", "g", "mul"))
_EPIW = (("exp", -1.0, 0.0), ("ln", 1.0, 1.0), ("exp", -1.0, 0.0), ("identity", math.exp(-0.5), 0.0))


def rwkv_branch(zs_lat, zs_ctx, inp):
    ZS = np.concatenate([zs_lat[0], zs_lat[1], zs_ctx[0], zs_ctx[1]], 0)
    split = lambda a: (a[:8192].reshape(2, 4096, -1), a[8192:].reshape(2, 256, -1))
    a_d, lw_d = [], []
    for d in range(2):
        wd = np.concatenate([ZS[:, 3072 + 64 * d:3136 + 64 * d], np.full((8704, 1), 20.0, np.float32)], 1)
        Bw = np.concatenate([inp["rw_w2"][0][d], inp["rw_w0"][0][d][None]], 0)
        lw_d.append(split(dev_mm(wd, Bw, fp32=True, a_act="tanh", epi=_EPIW)))
        ad = np.concatenate([ZS[:, 3200 + 64 * d:3264 + 64 * d], np.ones((8704, 1), np.float32)], 1)
        Ba = np.concatenate([inp["rw_a2"][0][d], inp["rw_a0"][0][d][None]], 0)
        a_d.append(split(dev_mm(ad, Ba, fp32=True, act="sigmoid")))
    g = dev_mm(ZS[:8192, 3328:3456], inp["rw_g2"][0], a_act="sigmoid")
    FR = np.zeros((2, 16, 2, 34, 64, 5, 128), np.float32)
    for b in range(2):
        for d in range(2):
            arrs = [_seq(zs_lat[b][:, 0:1024], zs_ctx[b][:, 0:1024], d), _seq(zs_lat[b][:, 1024:2048], zs_ctx[b][:, 1024:2048], d),
                    _seq(zs_lat[b][:, 2048:3072], zs_ctx[b][:, 2048:3072], d),
                    _seq(a_d[d][0][b], a_d[d][1][b], d), _seq(lw_d[d][0][b], lw_d[d][1][b], d)]
            for i, a in enumerate(arrs):
                FR[b, :, d, :, :, i, :] = a.reshape(34, 128, 16, 64).transpose(2, 0, 3, 1)
    PAR = np.stack([inp["rw_k_k"][0].reshape(16, 64), inp["rw_k_a"][0].reshape(16, 64), inp["rw_r_k"][0]], 2)
    PAR = np.broadcast_to(PAR[None, :, None], (2, 16, 2, 64, 3)).reshape(64, 64, 3)
    y, sb = dev_rwkv2(FR.reshape(64, 34, 64, 5, 128), PAR)
    y = y.reshape(2, 16, 2, 34 * 128, 64)[:, :, :, 256:]
    sb = sb.reshape(2, 16, 2, 34 * 128, 1)[:, :, :, 256:]
    yf = y[:, :, 0].transpose(0, 2, 1, 3).reshape(-1, 64); yb = y[:, :, 1, ::-1].transpose(0, 2, 1, 3).reshape(-1, 64)
    sf = sb[:, :, 0].transpose(0, 2, 1, 3).reshape(-1, 1); sbb = sb[:, :, 1, ::-1].transpose(0, 2, 1, 3).reshape(-1, 1)
    yn, ssum = dev_ew(_RWP1, {"yf": yf, "yb": yb, "sf": sf, "sb": sbb}, ["yn", "ssum"], {"ssum": 1})
    v_lat = zs_lat[:, :, 2048:3072].reshape(8192, 1024)
    (yrw,) = dev_ew(_RWP2, {"yn": yn.reshape(8192, 1024), "ss": np.repeat(ssum.reshape(8192, 16), 64, 1), "v": v_lat, "g": g,
                            "lw": inp["rw_ln_w"][0].reshape(1, 1024), "lb": inp["rw_ln_b"][0].reshape(1, 1024)}, ["o"])
    return yrw.reshape(2, 4096, 1024)


def phase_b(x, h_lat, gs5, yrw, f, s, inp):
    TC = 1024
    key = ("phaseb", TC)
    if key not in _cache:
        _cache[key] = _build_phaseb(TC)
    c32 = lambda a: np.ascontiguousarray(a, np.float32)
    wts = {"wg": c32(inp["w_in"][0][:, 4480:]), "wglu": c32(inp["s5_glu_w"][0]), "wrp": c32(inp["rw_proj"][0]), "wo": c32(inp["w_o"][0]),
           "w13": c32(inp["ffn_w13"][0]), "w2": c32(inp["ffn_w2"][0]), "ident": np.eye(128, dtype=np.float32)}
    maps = []
    for i in range(NCORES):
        b, q = i // 4, (i % 4) * TC
        m = dict(wts)
        m.update({"hT": c32(h_lat[b][q:q + TC].T), "ysT": c32(gs5[b][q:q + TC].T), "yrT": c32(yrw[b][q:q + TC].T),
                  "xr": c32(x[b][q:q + TC]), "vec": c32(np.stack([f[b, 2], s[b, 4], f[b, 3], f[b, 5], inp["norm_f"]]))})
        maps.append(m)
    r = _run(_cache[key], maps, ["out"])
    return np.stack([np.concatenate([r[4 * b + j][0] for j in range(4)], 0) for b in range(2)])


def mixer_inputs(inp):
    x, c, ctx, c_ctx = inp["x"], inp["c"], inp["ctx"], inp["c_ctx"]
    f, s = _adaln(c, c_ctx, inp["ada_w"], inp["ada_b"], inp["norm1_w"], inp["norm2_w"])
    nm = lambda rows, v: dev_ew(_NORM, {"x": np.ascontiguousarray(rows, np.float32), "sc": s[v, 1][None], "sh": f[v, 0][None]}, ["h"])[0]
    h_lat = np.stack([nm(x[0], 0), nm(x[1], 1)])
    h_ctx = nm(ctx.reshape(512, D), 2).reshape(2, 256, D)
    H = np.concatenate([h_lat.reshape(8192, D), h_ctx.reshape(512, D)], 0)
    z = dev_mm(H, inp["w_in"][0][:, :4480])
    return f, s, h_lat, z[:8192].reshape(2, 4096, 4480), z[8192:].reshape(2, 256, 4480)


def kernel(**inp):
    inp = {k: np.asarray(v) for k, v in inp.items()}
    f, s, h_lat, z_lat, z_ctx = mixer_inputs(inp)
    gs5, _ = s5_branch(np.ascontiguousarray(z_lat[..., :1024]), np.ascontiguousarray(z_ctx[..., :1024]), inp)
    zs_lat, zs_ctx = dev_shift(np.ascontiguousarray(z_lat[..., 1024:]), np.ascontiguousarray(z_ctx[..., 1024:]), inp["rw_mu"][0])
    yrw = rwkv_branch(zs_lat, zs_ctx, inp)
    return phase_b(inp["x"], h_lat, gs5, yrw, f, s, inp).astype(np.float32)


DFF = 5632


def _build_phaseb(TC, TB=256):
    p = Prog()
    hT = p.dram("hT", [D, TC], "ExternalInput")
    ysT = p.dram("ysT", [1024, TC], "ExternalInput")
    yrT = p.dram("yrT", [1024, TC], "ExternalInput")
    xr = p.dram("xr", [TC, D], "ExternalInput")
    vec = p.dram("vec", [5, D], "ExternalInput")
    ident = p.dram("ident", [128, 128], "ExternalInput")
    wg = p.dram("wg", [D, 2 * D], "ExternalInput")
    wglu = p.dram("wglu", [1024, 2 * D], "ExternalInput")
    wrp = p.dram("wrp", [1024, D], "ExternalInput")
    wo = p.dram("wo", [D, D], "ExternalInput")
    w13 = p.dram("w13", [D, 2 * DFF], "ExternalInput")
    w2 = p.dram("w2", [DFF, D], "ExternalInput")
    out = p.dram("out", [TC, D], "ExternalOutput")
    NT = TB // 128
    st32 = p.sb([128, 16, TB])
    h16 = p.sb([128, 16, TB], BF16); ys16 = p.sb([128, 8, TB], BF16); yr16 = p.sb([128, 8, TB], BF16)
    mh16 = p.sb([128, 16, TB], BF16)
    hid16 = p.sb([128, 44, TB], BF16)
    h1 = p.sb([128, NT, D])
    va = p.sb([128, D]); vb = p.sb([128, D])
    ws32 = p.sb([128, 16, 128]); ws16 = p.sb([128, 16, 128], BF16)
    wc32 = p.sb([128, 8, 512]); wc16 = p.sb([128, 8, 512], BF16)
    t32 = [p.sb([128, TB]) for _ in range(4)]
    xn = p.sb([128, D]); xn16 = p.sb([128, D], BF16); sq = p.sb([128, D])
    ss = p.sb([128, 1])
    id32 = p.sb([128, 128]); id16 = p.sb([128, 128], BF16)
    psA = p.ps([128, TB]); psB = p.ps([128, TB])
    psL = [p.ps([128, 512]) for _ in range(NT)]
    psT = p.ps([128, 128], BF16)
    p.dma(id32[:, :], ident[:, :])
    p.op("dve", lambda e: e.tensor_copy(out=id16[:, :], in_=id32[:, :]))

    def bvec(dst, i):
        p.dma(dst[:, :], vec[i:i + 1, :].partition_broadcast(128))

    def feat(ps, W, f0, x16, KT):
        p.dma(ws32[:, :KT, :], W[:, f0:f0 + 128].rearrange("(kt p) f -> p kt f", p=128))
        p.op("pool", lambda e: e.tensor_copy(out=ws16[:, :KT, :], in_=ws32[:, :KT, :]))
        for kt in range(KT):
            p.op("pe", lambda e, kt=kt: e.matmul(ps[:, :], ws16[:, kt, :], x16[:, kt, :], start=(kt == 0), stop=(kt == KT - 1)))

    def act(dst, src, func, scale=1.0, bias=0.0):
        p.op("act", lambda e: e.activation(out=dst, in_=src, func=func, scale=scale, bias=bias))

    def tt(dst, a, b, op):
        p.op("dve", lambda e: e.tensor_tensor(out=dst, in0=a, in1=b, op=op))

    def rstd(src):
        tt(sq[:, :], src, src, ALU.mult)
        p.op("dve", lambda e: e.tensor_reduce(out=ss[:, :], in_=sq[:, :], axis=AX.X, op=ALU.add))
        p.op("dve", lambda e: e.tensor_scalar(out=ss[:, :], in0=ss[:, :], scalar1=1.0 / D, scalar2=1e-6, op0=ALU.mult, op1=ALU.add))
        act(ss[:, :], ss[:, :], AF.Sqrt)
        p.op("dve", lambda e: e.reciprocal(out=ss[:, :], in_=ss[:, :]))

    def tmm(W, KT, src16, evac):
        for cb in range(4):
            for k0 in range(0, KT, 8):
                kn_ = min(8, KT - k0)
                p.dma(wc32[:, :kn_, :], W[k0 * 128:(k0 + kn_) * 128, cb * 512:(cb + 1) * 512].rearrange("(kt p) f -> p kt f", p=128))
                p.op("pool", lambda e, kn_=kn_: e.tensor_copy(out=wc16[:, :kn_, :], in_=wc32[:, :kn_, :]))
                for t in range(NT):
                    for kt in range(kn_):
                        p.op("pe", lambda e, t=t, kt=kt, k0=k0: e.matmul(
                            psL[t][:, :], src16[:, k0 + kt, t * 128:(t + 1) * 128], wc16[:, kt, :],
                            start=(k0 + kt == 0), stop=(k0 + kt == KT - 1)))
            for t in range(NT):
                evac(t, cb)

    for t0 in range(0, TC, TB):
        p.dma(st32[:, :, :], hT[:, t0:t0 + TB].rearrange("(kt p) t -> p kt t", p=128))
        p.op("dve", lambda e: e.tensor_copy(out=h16[:, :, :], in_=st32[:, :, :]))
        p.dma(st32[:, :8, :], ysT[:, t0:t0 + TB].rearrange("(kt p) t -> p kt t", p=128))
        p.op("dve", lambda e: e.tensor_copy(out=ys16[:, :, :], in_=st32[:, :8, :]))
        p.dma(st32[:, :8, :], yrT[:, t0:t0 + TB].rearrange("(kt p) t -> p kt t", p=128))
        p.op("dve", lambda e: e.tensor_copy(out=yr16[:, :, :], in_=st32[:, :8, :]))
        for j in range(16):
            feat(psA, wg, j * 128, h16, 16); act(t32[0][:, :], psA[:, :], AF.Sigmoid)
            feat(psA, wg, D + j * 128, h16, 16); act(t32[1][:, :], psA[:, :], AF.Sigmoid)
            feat(psA, wglu, j * 128, ys16, 8); act(t32[2][:, :], psA[:, :], AF.Copy)
            feat(psA, wglu, D + j * 128, ys16, 8); act(t32[3][:, :], psA[:, :], AF.Sigmoid)
            tt(t32[2][:, :], t32[2][:, :], t32[3][:, :], ALU.mult)
            tt(t32[2][:, :], t32[2][:, :], t32[0][:, :], ALU.mult)
            feat(psA, wrp, j * 128, yr16, 8)
            tt(t32[3][:, :], psA[:, :], t32[1][:, :], ALU.mult)
            tt(mh16[:, j, :], t32[2][:, :], t32[3][:, :], ALU.add)
        for t in range(NT):
            p.dma(h1[:, t, :], xr[t0 + t * 128:t0 + (t + 1) * 128, :])
        bvec(va, 0)

        def ev_lat(t, cb):
            cs = slice(cb * 512, (cb + 1) * 512)
            tt(sq[:, cs], psL[t][:, :], va[:, cs], ALU.mult)
            tt(h1[:, t, cs], h1[:, t, cs], sq[:, cs], ALU.add)
        tmm(wo, 16, mh16, ev_lat)
        bvec(va, 1); bvec(vb, 2)
        for t in range(NT):
            rstd(h1[:, t, :])
            p.op("dve", lambda e, t=t: e.tensor_scalar(out=xn[:, :], in0=h1[:, t, :], scalar1=ss[:, 0:1], scalar2=None, op0=ALU.mult))
            tt(xn[:, :], xn[:, :], va[:, :], ALU.mult)
            tt(xn16[:, :], xn[:, :], vb[:, :], ALU.add)
            for k in range(16):
                p.op("pe", lambda e, k=k: e.transpose(psT[:, :], xn16[:, k * 128:(k + 1) * 128], id16[:, :]))
                p.op("act", lambda e, k=k, t=t: e.copy(out=mh16[:, k, t * 128:(t + 1) * 128], in_=psT[:, :]))
        for f in range(44):
            feat(psA, w13, f * 128, mh16, 16); act(t32[0][:, :], psA[:, :], AF.Silu)
            feat(psB, w13, DFF + f * 128, mh16, 16)
            tt(hid16[:, f, :], psB[:, :], t32[0][:, :], ALU.mult)
        bvec(va, 3)

        def ev_ffn(t, cb):
            cs = slice(cb * 512, (cb + 1) * 512)
            tt(sq[:, cs], psL[t][:, :], va[:, cs], ALU.mult)
            tt(h1[:, t, cs], h1[:, t, cs], sq[:, cs], ALU.add)
        tmm(w2, 44, hid16, ev_ffn)
        bvec(va, 4)
        for t in range(NT):
            rstd(h1[:, t, :])
            p.op("dve", lambda e, t=t: e.tensor_scalar(out=xn[:, :], in0=h1[:, t, :], scalar1=ss[:, 0:1], scalar2=None, op0=ALU.mult))
            tt(xn[:, :], xn[:, :], va[:, :], ALU.mult)
            p.dma(out[t0 + t * 128:t0 + (t + 1) * 128, :], xn[:, :])
    return p.finish()


def _build_shift(Rc, F):
    p = Prog()
    zp = p.dram("zp", [Rc + 128, F], "ExternalInput")
    msk = p.dram("msk", [Rc, 5], "ExternalInput")
    mu = p.dram("mu", [1, F], "ExternalInput")
    zs = p.dram("zs", [Rc, F], "ExternalOutput")
    zc = p.sb([128, F]); zn = p.sb([128, F]); acc = p.sb([128, F]); mub = p.sb([128, F]); m = p.sb([128, 5])
    p.dma(mub[:, :], mu[0:1, :].partition_broadcast(128))
    for t in range(Rc // 128):
        r0 = t * 128
        p.dma(m[:, :], msk[r0:r0 + 128, :])
        p.dma(zc[:, :], zp[64 + r0:64 + r0 + 128, :])
        for i, off in enumerate((63, 65, 0, 128)):
            p.dma(zn[:, :], zp[off + r0:off + r0 + 128, :])
            if i == 0:
                p.op("dve", lambda e: e.tensor_scalar(out=acc[:, :], in0=zn[:, :], scalar1=m[:, 0:1], scalar2=None, op0=ALU.mult))
            else:
                p.op("dve", lambda e, i=i: e.scalar_tensor_tensor(out=acc[:, :], in0=zn[:, :], scalar=m[:, i:i + 1], in1=acc[:, :], op0=ALU.mult, op1=ALU.add))
        p.op("dve", lambda e: e.scalar_tensor_tensor(out=acc[:, :], in0=acc[:, :], scalar=m[:, 4:5], in1=zc[:, :], op0=ALU.mult, op1=ALU.subtract))
        p.op("dve", lambda e: e.tensor_tensor(out=acc[:, :], in0=acc[:, :], in1=mub[:, :], op=ALU.mult))
        p.op("dve", lambda e: e.tensor_tensor(out=acc[:, :], in0=acc[:, :], in1=zc[:, :], op=ALU.add))
        p.dma(zs[r0:r0 + 128, :], acc[:, :])
    return p.finish()


def dev_shift(z_lat, z_ctx, mu):
    F = z_lat.shape[-1]
    seqs = [z_lat[0], z_lat[1], z_ctx[0], z_ctx[1]]
    pad = np.zeros((64, F), np.float32)
    P = np.concatenate([np.zeros((64, F), np.float32)] + [a for s_ in seqs for a in (pad, s_, pad)] + [np.zeros((64, F), np.float32)], 0)
    M = []
    i = np.arange(4096); r_, c_ = i // 64, i % 64
    ml = np.stack([c_ > 0, c_ < 63, r_ > 0, r_ < 63], 1).astype(np.float32)
    ml = np.concatenate([ml, 1.0 / ml.sum(1, keepdims=True)], 1)
    j = np.arange(256)
    mc = np.stack([j > 0, j < 255, j < 0, j < 0], 1).astype(np.float32)
    mc = np.concatenate([mc, 1.0 / mc.sum(1, keepdims=True)], 1)
    z5 = np.zeros((64, 5), np.float32)
    for mm_ in (ml, ml, mc, mc):
        M += [z5, mm_, z5]
    M = np.concatenate(M, 0).astype(np.float32)
    R = M.shape[0]
    Rc = R // NCORES
    key = ("shift", Rc, F)
    if key not in _cache:
        _cache[key] = _build_shift(Rc, F)
    maps = [{"zp": np.ascontiguousarray(P[i * Rc:i * Rc + Rc + 128]), "msk": np.ascontiguousarray(M[i * Rc:(i + 1) * Rc]),
             "mu": np.ascontiguousarray(mu.reshape(1, F), np.float32)} for i in range(NCORES)]
    r = _run(_cache[key], maps, ["zs"])
    out = np.concatenate([x[0] for x in r], 0)
    o = 0
    res = []
    for n in (4096, 4096, 256, 256):
        res.append(out[o + 64:o + 64 + n])
        o += n + 128
    return np.stack(res[:2]), np.stack(res[2:])


def _build_rwkv2(nch, NCH):
    p = Prog()
    fr = p.dram("fr", [nch, NCH, 64, 5, 128], "ExternalInput")
    par = p.dram("par", [nch, 64, 3], "ExternalInput")
    cst = p.dram("cst", [128, 4, 128], "ExternalInput")
    yo = p.dram("y", [nch, NCH, 128, 64], "ExternalOutput")
    so = p.dram("s", [nch, NCH, 128, 1], "ExternalOutput")
    C = p.sb([128, 4, 128])
    R = p.sb([64, 5, 128])
    pr = p.sb([64, 3]); omk = p.sb([64, 1]); ntot = p.sb([64, 1]); g = p.sb([64, 1])
    ones = p.sb([64, 128]); ones16 = p.sb([64, 64], BF16); id16 = p.sb([64, 64], BF16)
    ncs = p.sb([64, 128]); E = [p.sb([64, 128]) for _ in range(4)]
    kkn = p.sb([64, 128]); kd = p.sb([64, 128]); na = p.sb([64, 128]); tmp = p.sb([64, 128]); tmp2 = p.sb([64, 128])
    fm16 = p.sb([64, 4, 128], BF16)
    kp16 = p.sb([64, 128], BF16); ap16 = p.sb([64, 128], BF16); v16 = p.sb([64, 128], BF16); rk16 = p.sb([64, 128], BF16)
    tm16 = p.sb([128, 3, 64], BF16)
    P16 = p.sb([128, 128]); PT16 = p.sb([128, 128])
    Pn16 = p.sb([128, 128]); PTn16 = p.sb([128, 128])
    Y32 = p.sb([128, 128]); Y16 = p.sb([128, 128], BF16)
    MT16 = p.sb([128, 128], BF16); ArT16 = p.sb([128, 128], BF16); AaT16 = p.sb([128, 128], BF16)
    W16 = p.sb([128, 64], BF16); U16 = p.sb([128, 64], BF16); yt = p.sb([128, 64]); st = p.sb([128, 1])
    H32 = p.sb([64, 64]); H16 = p.sb([64, 64], BF16)
    ps = p.ps([128, 128]); ps2 = p.ps([128, 64]); ps3 = p.ps([64, 128]); psT = p.ps([128, 64], BF16); ps4 = p.ps([128, 1])
    p.dma(C[:, :, :], cst[:, :, :])
    p.op("dve", lambda e: e.memset(ones[:, :], 1.0))
    p.op("dve", lambda e: e.memset(ones16[:, :], 1.0))
    p.op("dve", lambda e: e.tensor_copy(out=id16[:, :], in_=C[0:64, 3, 0:64]))
    RH, KH, AH, BH = 0, 1, 2, 3

    def mm(out, lhsT, rhs, start=True, stop=True):
        p.op("pe", lambda e: e.matmul(out, lhsT, rhs, start=start, stop=stop))

    def masked(dst, mi):
        p.op("dve", lambda e: e.tensor_tensor(out=dst[:, :], in0=ps[:, :], in1=C[:, mi, :], op=ALU.mult))

    def tt(dst, a, b, op=ALU.mult):
        p.op("dve", lambda e: e.tensor_tensor(out=dst, in0=a, in1=b, op=op))

    def act(dst, src, func, scale=1.0, bias=0.0):
        p.op("act", lambda e: e.activation(out=dst, in_=src, func=func, scale=scale, bias=bias))

    for ci in range(nch):
        p.dma(pr[:, :], par[ci])
        p.op("dve", lambda e: e.tensor_scalar(out=omk[:, :], in0=pr[:, 1:2], scalar1=-1.0, scalar2=1.0, op0=ALU.mult, op1=ALU.add))
        p.op("dve", lambda e: e.memset(H32[:, :], 0.0))
        p.op("dve", lambda e: e.memset(H16[:, :], 0.0))
        for c in range(NCH):
            p.dma(R[:, :, :], fr[ci, c])
            r_, k_, v_, a_, lwp = (R[:, i, :] for i in range(5))
            p.op("dve", lambda e: e.tensor_tensor_scan(out=ncs[:, :], data0=ones[:, :], data1=lwp, initial=0.0, op0=ALU.mult, op1=ALU.add))
            p.op("dve", lambda e: e.tensor_scalar(out=ntot[:, :], in0=ncs[:, 127:128], scalar1=-1.0, scalar2=None, op0=ALU.mult))
            act(E[0][:, :], ncs[:, :], AF.Exp, scale=-1.0)
            act(E[1][:, :], ncs[:, :], AF.Exp, scale=1.0)
            tt(tmp[:, :], lwp, ncs[:, :], ALU.subtract)
            act(E[2][:, :], tmp[:, :], AF.Exp)
            act(E[3][:, :], ncs[:, :], AF.Exp, scale=1.0, bias=ntot[:, 0:1])
            act(g[:, :], ntot[:, :], AF.Exp)
            p.op("dve", lambda e: e.tensor_scalar(out=kkn[:, :], in0=k_, scalar1=pr[:, 0:1], scalar2=None, op0=ALU.mult))
            tt(tmp[:, :], kkn[:, :], kkn[:, :])
            mm(ps3[:, :], ones[:, 0:64], tmp[:, :])
            p.op("dve", lambda e: e.tensor_scalar(out=tmp[:, :], in0=ps3[:, :], scalar1=1e-12, scalar2=None, op0=ALU.add))
            act(tmp[:, :], tmp[:, :], AF.Sqrt)
            p.op("dve", lambda e: e.reciprocal(out=tmp[:, :], in_=tmp[:, :]))
            tt(kkn[:, :], kkn[:, :], tmp[:, :])
            p.op("dve", lambda e: e.tensor_scalar(out=tmp[:, :], in0=a_, scalar1=pr[:, 1:2], scalar2=omk[:, 0:1], op0=ALU.mult, op1=ALU.add))
            tt(kd[:, :], k_, tmp[:, :])
            p.op("dve", lambda e: e.scalar_tensor_tensor(out=na[:, :], in0=kkn[:, :], scalar=-1.0, in1=a_, op0=ALU.mult, op1=ALU.mult))
            tt(fm16[:, RH, :], r_, E[0][:, :]); tt(fm16[:, KH, :], kd[:, :], E[1][:, :])
            tt(fm16[:, AH, :], na[:, :], E[1][:, :]); tt(fm16[:, BH, :], kkn[:, :], E[2][:, :])
            tt(kp16[:, :], kd[:, :], E[3][:, :]); tt(ap16[:, :], na[:, :], E[3][:, :])
            p.op("dve", lambda e: e.tensor_copy(out=v16[:, :], in_=v_))
            tt(tmp2[:, :], r_, kd[:, :])
            p.op("dve", lambda e: e.tensor_scalar(out=rk16[:, :], in0=tmp2[:, :], scalar1=pr[:, 2:3], scalar2=None, op0=ALU.mult))
            mm(ps4[:, :], rk16[:, :], ones16[:, 0:1])
            p.op("act", lambda e: e.copy(out=st[:, :], in_=ps4[:, :]))
            p.dma(so[ci, c], st[:, :])
            for i, src in enumerate((v16, kp16, ap16)):
                p.op("pe", lambda e, src=src: e.transpose(psT[:, :], src[:, :], id16[:, :]))
                p.op("act", lambda e, i=i: e.copy(out=tm16[:, i, :], in_=psT[:, :]))
            mm(ps[:, :], fm16[:, AH, :], fm16[:, BH, :]); masked(PT16, 0)
            p.op("dve", lambda e: e.tensor_tensor(out=Y32[:, :], in0=PT16[:, :], in1=C[:, 3, :], op=ALU.add))
            mm(ps[:, :], fm16[:, BH, :], fm16[:, AH, :]); masked(P16, 1)
            mm(ps[:, :], fm16[:, KH, :], fm16[:, BH, :]); masked(MT16, 0)
            mm(ps[:, :], fm16[:, KH, :], fm16[:, RH, :]); masked(ArT16, 2)
            mm(ps[:, :], fm16[:, AH, :], fm16[:, RH, :]); masked(AaT16, 2)
            P, PT, Pn, PTn = P16, PT16, Pn16, PTn16
            for lvl in range(6):
                if lvl < 5:
                    mm(ps[:, :], P[:, :], PT[:, :])
                    p.op("act", lambda e, PTn=PTn: e.copy(out=PTn[:, :], in_=ps[:, :]))
                mm(ps[:, :], PT[:, :], P[:, :])
                p.op("act", lambda e, Pn=Pn: e.copy(out=Pn[:, :], in_=ps[:, :]))
                P, Pn = Pn, P
                PT, PTn = PTn, PT
                mm(ps[:, :], P[:, :], Y32[:, :])
                p.op("dve", lambda e: e.tensor_tensor(out=Y32[:, :], in0=ps[:, :], in1=Y32[:, :], op=ALU.add))
            p.op("dve", lambda e: e.tensor_copy(out=Y16[:, :], in_=Y32[:, :]))
            mm(ps2[:, :], fm16[:, BH, :], H16[:, :], True, False)
            mm(ps2[:, :], MT16[:, :], tm16[:, 0, :], False, True)
            p.op("act", lambda e: e.copy(out=W16[:, :], in_=ps2[:, :]))
            mm(ps2[:, :], Y16[:, :], W16[:, :])
            p.op("act", lambda e: e.copy(out=U16[:, :], in_=ps2[:, :]))
            mm(ps2[:, :], fm16[:, RH, :], H16[:, :], True, False)
            mm(ps2[:, :], ArT16[:, :], tm16[:, 0, :], False, False)
            mm(ps2[:, :], AaT16[:, :], U16[:, :], False, True)
            p.op("act", lambda e: e.copy(out=yt[:, :], in_=ps2[:, :]))
            p.dma(yo[ci, c], yt[:, :])
            mm(ps3[:, 0:64], tm16[:, 1, :], tm16[:, 0, :], True, False)
            mm(ps3[:, 0:64], tm16[:, 2, :], U16[:, :], False, True)
            p.op("dve", lambda e: e.scalar_tensor_tensor(out=H32[:, :], in0=H32[:, :], scalar=g[:, 0:1],
                                                          in1=ps3[:, 0:64], op0=ALU.mult, op1=ALU.add))
            p.op("dve", lambda e: e.tensor_copy(out=H16[:, :], in_=H32[:, :]))
    return p.finish()


def dev_rwkv2(FR, PAR):
    nchains, NCH = FR.shape[:2]
    assert nchains % NCORES == 0
    nch = nchains // NCORES
    key = ("rwkv2", nch, NCH)
    if key not in _cache:
        _cache[key] = _build_rwkv2(nch, NCH)
    cst = _rwkv_consts()
    maps = [{"fr": np.ascontiguousarray(FR[i * nch:(i + 1) * nch], np.float32),
             "par": np.ascontiguousarray(PAR[i * nch:(i + 1) * nch], np.float32), "cst": cst} for i in range(NCORES)]
    r = _run(_cache[key], maps, ["y", "s"])
    return np.concatenate([x[0] for x in r], 0), np.concatenate([x[1] for x in r], 0)
```
